# Optimizing a Trainium2 kernel written in Bass

```python
import jax, jax.numpy as jnp
from jax import lax
import numpy as np

D_MODEL = 4096
BATCH = 2
SEQ = 4096
DEPTH = 2

RET_HEAD_DIM = 256
RET_WIDTH = D_MODEL // 2
RET_HEADS = RET_WIDTH // RET_HEAD_DIM
RET_CHUNK = 128
HGRN_HEAD_DIM = 128
HGRN_WIDTH = D_MODEL // 4
HGRN_HEADS = HGRN_WIDTH // HGRN_HEAD_DIM
HGRN_CHUNK = 64
FOURIER_GROUP_DIM = 128
FOURIER_WIDTH = D_MODEL - RET_WIDTH - HGRN_WIDTH
FOURIER_GROUPS = FOURIER_WIDTH // FOURIER_GROUP_DIM
MIX_WIDTH = RET_WIDTH + HGRN_WIDTH + FOURIER_WIDTH
IN_PROJ_WIDTH = 4 * RET_WIDTH + 5 * HGRN_WIDTH + FOURIER_WIDTH
XATTN_HEADS = 4
XATTN_HEAD_DIM = 256
XATTN_WIDTH = XATTN_HEADS * XATTN_HEAD_DIM
MEM_TOKENS = 256
FFN_HIDDEN = ((8 * D_MODEL // 3 + 255) // 256) * 256
ROPE_BASE = 10000.0
RMS_EPS = 1e-6
GN_EPS = 1e-5

kernel_name = "hybrid_retention_hgrn2_fnet_encoder"


def _rms_norm(x, g, eps=RMS_EPS):
    xf = x.astype(jnp.float32)
    y = xf * lax.rsqrt(jnp.mean(xf * xf, axis=-1, keepdims=True) + eps)
    return (y * g.astype(jnp.float32)).astype(x.dtype)


def _to_heads(t, n_heads):
    b, s, w = t.shape
    return t.reshape(b, s, n_heads, w // n_heads).transpose(0, 2, 1, 3)


def _from_heads(t):
    b, h, s, d = t.shape
    return t.transpose(0, 2, 1, 3).reshape(b, s, h * d)


def _to_chunks(t, c):
    b, h, s, d = t.shape
    return jnp.moveaxis(t.reshape(b, h, s // c, c, d), 2, 0)


def _from_chunks(o):
    n, b, h, c, d = o.shape
    return jnp.moveaxis(o, 0, 2).reshape(b, h, n * c, d)


def _rotary(t):
    s, d = t.shape[2], t.shape[3]
    half = d // 2
    inv = jnp.power(ROPE_BASE, -jnp.arange(half, dtype=jnp.float32) / half)
    ang = jnp.arange(s, dtype=jnp.float32)[:, None] * inv[None, :]
    cos, sin = jnp.cos(ang), jnp.sin(ang)
    t1, t2 = t[..., :half], t[..., half:]
    return jnp.concatenate([t1 * cos - t2 * sin, t1 * sin + t2 * cos], axis=-1)


def _retention_one_dir(q, k, v, gamma, inclusive):
    b, h, s, d = q.shape
    dv = v.shape[-1]
    c = RET_CHUNK
    log_g = jnp.log(gamma)
    idx = jnp.arange(c, dtype=jnp.float32)
    rel = idx[:, None] - idx[None, :]
    mask = (rel >= 0) if inclusive else (rel > 0)
    decay = jnp.where(mask[None], jnp.exp(jnp.where(mask, rel, 0.0)[None] * log_g[:, None, None]), 0.0)
    q_dec = jnp.exp((idx + 1.0)[None, :] * log_g[:, None])[None, :, :, None]
    k_dec = jnp.exp((c - 1.0 - idx)[None, :] * log_g[:, None])[None, :, :, None]
    chunk_dec = jnp.exp(c * log_g)[None, :, None, None]

    def step(state, inp):
        qc, kc, vc = inp
        scores = jnp.einsum('bhid,bhjd->bhij', qc, kc) * decay[None]
        o = (jnp.einsum('bhij,bhje->bhie', scores, vc)
             + jnp.einsum('bhid,bhde->bhie', qc * q_dec, state))
        state = chunk_dec * state + jnp.einsum('bhjd,bhje->bhde', kc * k_dec, vc)
        return state, o

    state0 = jnp.zeros((b, h, d, dv), jnp.float32)
    _, o = lax.scan(step, state0, (_to_chunks(q, c), _to_chunks(k, c), _to_chunks(v, c)))
    return _from_chunks(o)


def _retention_mixer(rq, rk, rv, rg, norm_g):
    f32 = jnp.float32
    q = _rotary(_to_heads(rq, RET_HEADS).astype(f32)) * (RET_HEAD_DIM ** -0.5)
    k = _rotary(_to_heads(rk, RET_HEADS).astype(f32))
    v = _to_heads(rv, RET_HEADS).astype(f32)
    gamma_fwd = 1.0 - jnp.power(2.0, -5.0 - jnp.arange(RET_HEADS, dtype=f32))
    gamma_bwd = gamma_fwd[::-1]
    o_fwd = _retention_one_dir(q, k, v, gamma_fwd, True)
    o_bwd = jnp.flip(_retention_one_dir(jnp.flip(q, 2), jnp.flip(k, 2), jnp.flip(v, 2), gamma_bwd, False), 2)
    o = o_fwd + o_bwd
    mu = jnp.mean(o, axis=-1, keepdims=True)
    var = jnp.mean(jnp.square(o - mu), axis=-1, keepdims=True)
    o = _from_heads((o - mu) * lax.rsqrt(var + GN_EPS)) * norm_g.astype(f32)
    return (jax.nn.silu(rg.astype(f32)) * o).astype(rq.dtype)


def _gla_one_dir(q, k, v, log_f):
    b, h, s, dk = q.shape
    dv = v.shape[-1]
    c = HGRN_CHUNK
    tri = jnp.tril(jnp.ones((c, c), dtype=bool))

    def step(state, inp):
        qc, kc, vc, gc = inp
        cum = jnp.cumsum(gc, axis=2)
        rel = cum[:, :, :, None, :] - cum[:, :, None, :, :]
        w = jnp.exp(jnp.where(tri[None, None, :, :, None], rel, -jnp.inf))
        scores = jnp.einsum('bhtd,bhsd,bhtsd->bhts', qc, kc, w)
        last = cum[:, :, -1, :]
        o = (jnp.einsum('bhts,bhse->bhte', scores, vc)
             + jnp.einsum('bhtd,bhde->bhte', qc * jnp.exp(cum), state))
        state = (jnp.exp(last)[..., None] * state
                 + jnp.einsum('bhsd,bhse->bhde', kc * jnp.exp(last[:, :, None, :] - cum), vc))
        return state, o

    state0 = jnp.zeros((b, h, dk, dv), jnp.float32)
    _, o = lax.scan(step, state0, (_to_chunks(q, c), _to_chunks(k, c), _to_chunks(v, c), _to_chunks(log_f, c)))
    return _from_chunks(o)


def _hgrn2_mixer(hq, hf_fwd, hf_bwd, hi, hg, lb, norm_g):
    f32 = jnp.float32
    q = _to_heads(hq, HGRN_HEADS).astype(f32) * (HGRN_HEAD_DIM ** -0.5)
    i = _to_heads(hi, HGRN_HEADS).astype(f32)

    def gates(z, lb_dir):
        lb_h = lb_dir.reshape(HGRN_HEADS, 1, HGRN_HEAD_DIM)
        f = lb_h + (1.0 - lb_h) * jax.nn.sigmoid(_to_heads(z, HGRN_HEADS).astype(f32))
        return 1.0 - f, jnp.log(f)

    k_f, logf_f = gates(hf_fwd, lb[0])
    k_b, logf_b = gates(hf_bwd, lb[1])
    o_fwd = _gla_one_dir(q, k_f, i, logf_f)
    o_bwd = jnp.flip(_gla_one_dir(jnp.flip(q, 2), jnp.flip(k_b, 2), jnp.flip(i, 2), jnp.flip(logf_b, 2)), 2)
    o = o_fwd + o_bwd
    o = o * lax.rsqrt(jnp.mean(o * o, axis=-1, keepdims=True) + RMS_EPS)
    o = _from_heads(o) * norm_g.astype(f32)
    return (jax.nn.silu(hg.astype(f32)) * o).astype(hq.dtype)


def _fourier_mixer(z, w, bias):
    b, s, _ = z.shape
    zg = z.astype(jnp.float32).reshape(b, s, FOURIER_GROUPS, FOURIER_GROUP_DIM)
    spec = jnp.real(jnp.fft.fft2(zg, axes=(1, 3), norm='ortho'))
    y = jnp.einsum('bsgc,gcd->bsgd', spec, w.astype(jnp.float32)) + bias.astype(jnp.float32)
    return y.reshape(b, s, FOURIER_WIDTH).astype(z.dtype)


def _cross_attention(c, mem_n, wq, wk, wv, wo):
    f32 = jnp.float32
    q = _to_heads(c @ wq, XATTN_HEADS).astype(f32) * (XATTN_HEAD_DIM ** -0.5)
    k = _to_heads(mem_n @ wk, XATTN_HEADS).astype(f32)
    v = _to_heads(mem_n @ wv, XATTN_HEADS).astype(f32)
    p = jax.nn.softmax(jnp.einsum('bhsd,bhmd->bhsm', q, k), axis=-1)
    o = jnp.einsum('bhsm,bhmd->bhsd', p, v)
    return _from_heads(o).astype(c.dtype) @ wo


def setup_inputs(seed: int = 0) -> dict:
    key = jax.random.key(seed)
    ks = jax.random.split(key, 24)
    L, D = DEPTH, D_MODEL
    f32 = jnp.float32

    def nrm(k, shape, scale):
        return jax.random.normal(k, shape, f32) * scale

    def gain(k, shape):
        return 1.0 + 0.05 * jax.random.normal(k, shape, f32)

    return {
        "x": jax.random.normal(ks[0], (BATCH, SEQ, D), f32),
        "mem": jax.random.normal(ks[1], (BATCH, MEM_TOKENS, D), f32),
        "mem_norm_g": gain(ks[2], (D,)),
        "pre_mix_g": gain(ks[3], (L, D)),
        "w_in": nrm(ks[4], (L, D, IN_PROJ_WIDTH), D ** -0.5),
        "ret_norm_g": gain(ks[5], (L, RET_WIDTH)),
        "hgrn_lb_logits": 1.0 + 0.1 * jax.random.normal(ks[6], (L, 2, HGRN_WIDTH), f32),
        "hgrn_norm_g": gain(ks[7], (L, HGRN_WIDTH)),
        "fourier_w": nrm(ks[8], (L, FOURIER_GROUPS, FOURIER_GROUP_DIM, FOURIER_GROUP_DIM), FOURIER_GROUP_DIM ** -0.5),
        "fourier_b": nrm(ks[9], (L, FOURIER_GROUPS, FOURIER_GROUP_DIM), 0.02),
        "w_out": nrm(ks[10], (L, MIX_WIDTH, D), MIX_WIDTH ** -0.5),
        "post_mix_g": gain(ks[11], (L, D)),
        "pre_xattn_g": gain(ks[12], (L, D)),
        "xattn_wq": nrm(ks[13], (L, D, XATTN_WIDTH), D ** -0.5),
        "xattn_wk": nrm(ks[14], (L, D, XATTN_WIDTH), D ** -0.5),
        "xattn_wv": nrm(ks[15], (L, D, XATTN_WIDTH), D ** -0.5),
        "xattn_wo": nrm(ks[16], (L, XATTN_WIDTH, D), XATTN_WIDTH ** -0.5),
        "post_xattn_g": gain(ks[17], (L, D)),
        "pre_ffn_g": gain(ks[18], (L, D)),
        "ffn_w_gate": nrm(ks[19], (L, D, FFN_HIDDEN), D ** -0.5),
        "ffn_w_up": nrm(ks[20], (L, D, FFN_HIDDEN), D ** -0.5),
        "ffn_w_down": nrm(ks[21], (L, FFN_HIDDEN, D), FFN_HIDDEN ** -0.5),
        "post_ffn_g": gain(ks[22], (L, D)),
    }


def reference(x, mem, mem_norm_g, pre_mix_g, w_in, ret_norm_g, hgrn_lb_logits, hgrn_norm_g,
              fourier_w, fourier_b, w_out, post_mix_g, pre_xattn_g, xattn_wq, xattn_wk, xattn_wv,
              xattn_wo, post_xattn_g, pre_ffn_g, ffn_w_gate, ffn_w_up, ffn_w_down, post_ffn_g):
    sizes = [RET_WIDTH] * 4 + [HGRN_WIDTH] * 5 + [FOURIER_WIDTH]
    splits = [int(v) for v in np.cumsum(sizes)[:-1]]
    p = jax.nn.softmax(hgrn_lb_logits.astype(jnp.float32), axis=0)
    lower_bounds = jnp.cumsum(p, axis=0) - p[0:1]
    mem_n = _rms_norm(mem, mem_norm_g)

    h = x
    for l in range(DEPTH):
        a = _rms_norm(h, pre_mix_g[l])
        proj = a @ w_in[l]
        rq, rk, rv, rg, hq, hff, hfb, hi, hg, fz = jnp.split(proj, splits, axis=-1)
        y_ret = _retention_mixer(rq, rk, rv, rg, ret_norm_g[l])
        y_hgrn = _hgrn2_mixer(hq, hff, hfb, hi, hg, lower_bounds[l], hgrn_norm_g[l])
        y_fft = _fourier_mixer(fz, fourier_w[l], fourier_b[l])
        mixed = jnp.concatenate([y_ret, y_hgrn, y_fft], axis=-1).astype(h.dtype) @ w_out[l]
        h = h + _rms_norm(mixed, post_mix_g[l])
        c = _rms_norm(h, pre_xattn_g[l])
        xa = _cross_attention(c, mem_n, xattn_wq[l], xattn_wk[l], xattn_wv[l], xattn_wo[l])
        h = h + _rms_norm(xa, post_xattn_g[l])
        f = _rms_norm(h, pre_ffn_g[l])
        ff = (jax.nn.silu(f @ ffn_w_gate[l]) * (f @ ffn_w_up[l])) @ ffn_w_down[l]
        h = h + _rms_norm(ff, post_ffn_g[l])
    return h
```

```python
import numpy as np
import ml_dtypes
from contextlib import ExitStack
import concourse.bass as bass
import concourse.mybir as mybir
from concourse.bass_utils import run_bass_kernel_spmd

F32 = mybir.dt.float32
BF16 = mybir.dt.bfloat16
AF = mybir.ActivationFunctionType
ALU = mybir.AluOpType
AX = mybir.AxisListType

D = 4096
DC = 32
NPROJ = 14336
FFN = 11008
FC = 86
RMS_EPS = 1e-6
GN_EPS = 1e-5
O_RQ, O_RK, O_RV, O_RG = 0, 2048, 4096, 6144
O_HQ, O_HFF, O_HFB, O_HI, O_HG = 8192, 9216, 10240, 11264, 12288
O_FZ = 13312
G_MEM, G_PREMIX, G_POSTMIX, G_PREX, G_POSTX, G_PREF, G_POSTF = 0, 1, 2, 3, 4, 5, 6


class Buf:
    def __init__(self, t, multi=False):
        self.t = t
        self.w = {}
        self.r = {}
        self.prev = {}
        self.mode = 'w'
        self.multi = multi

    def __getitem__(self, idx):
        return self.t[idx]


def _merge(dst, src):
    for k, v in src.items():
        if dst.get(k, 0) < v:
            dst[k] = v


class KB:
    def __init__(self, nc):
        self.nc = nc
        self.es = ExitStack()
        self.eng = {'pe': nc.tensor, 'act': nc.scalar, 'dve': nc.vector,
                    'pool': nc.gpsimd, 'sp': nc.sync}
        self.esem = {e: self.es.enter_context(nc.semaphore('sem_' + e)) for e in self.eng}
        self.ecnt = {e: 0 for e in self.eng}
        self.known = {e: {} for e in self.eng}
        self.dsem = {}
        self.dcnt = {}

    def semof(self, k):
        return self.esem[k[1]] if k[0] == 'e' else self.dsem[k[1]]

    def _deps(self, reads, writes):
        need = {}
        for b in reads:
            if b.mode == 'w':
                b.mode = 'r'
            _merge(need, b.w)
        for b in writes:
            if b.mode == 'r':
                b.prev = {}
                _merge(b.prev, b.w)
                _merge(b.prev, b.r)
                b.w = {}
                b.r = {}
                b.mode = 'w'
            _merge(need, b.prev)
            _merge(need, b.r)
            if not b.multi:
                _merge(need, b.w)
        return need

    def _wait(self, e, need):
        eng = self.eng[e]
        kn = self.known[e]
        for k, v in need.items():
            if k == ('e', e):
                continue
            if kn.get(k, 0) >= v:
                continue
            eng.wait_ge(self.semof(k), v)
            kn[k] = v

    def _mark(self, ev, reads, writes):
        k, v = ev
        for b in reads:
            if b.r.get(k, 0) < v:
                b.r[k] = v
        for b in writes:
            if b.w.get(k, 0) < v:
                b.w[k] = v

    def op(self, e, fn, reads=(), writes=()):
        raw_self = 0
        if e != 'pe':
            for b in reads:
                raw_self = max(raw_self, b.w.get(('e', e), 0))
        need = self._deps(reads, writes)
        self._wait(e, need)
        if raw_self > self.known[e].get(('e', e), 0):
            self.eng[e].wait_ge(self.esem[e], raw_self)
            self.known[e][('e', e)] = raw_self
        ins = fn(self.eng[e])
        self.ecnt[e] += 1
        ins.then_inc(self.esem[e], 1)
        self._mark((('e', e), self.ecnt[e]), reads, writes)

    def dma(self, q, key, out, in_, reads=(), writes=()):
        if key not in self.dsem:
            self.dsem[key] = self.es.enter_context(self.nc.semaphore('dq_' + key))
            self.dcnt[key] = 0
        need = self._deps(reads, writes)
        if self.dcnt[key] > 0:
            need[('d', key)] = self.dcnt[key]
        self._wait(q, need)
        ins = self.eng[q].dma_start(out=out, in_=in_)
        self.dcnt[key] += 16
        ins.then_inc(self.dsem[key], 16)
        self._mark((('d', key), self.dcnt[key]), reads, writes)

    def barrier(self):
        need = {}
        for e in self.eng:
            if self.ecnt[e] > 0:
                need[('e', e)] = self.ecnt[e]
        for k, c in self.dcnt.items():
            if c > 0:
                need[('d', k)] = c
        for e in self.eng:
            self._wait(e, need)


class Stage:
    def __init__(self, kb):
        self.kb = kb
        self.st = ExitStack()

    def __enter__(self):
        return self

    _uid = [0]

    def sb(self, name, shape, dt, multi=False):
        Stage._uid[0] += 1
        name = f"{name}_{Stage._uid[0]}"
        return Buf(self.st.enter_context(self.kb.nc.sbuf_tensor(name, list(shape), dt)), multi)

    def ps(self, name, shape, dt=F32):
        Stage._uid[0] += 1
        name = f"{name}_{Stage._uid[0]}"
        return Buf(self.st.enter_context(self.kb.nc.psum_tensor(name, list(shape), dt)))

    def __exit__(self, *a):
        self.kb.barrier()
        self.st.close()
        return False


def build_program(S, L, debug=False, stop_after=None):
    assert S % 512 == 0
    nc = bass.Bass("TRN2", target_bir_lowering=False)
    kb = KB(nc)
    NT = S // 128
    NG = S // 512

    specs = {
        "x": ([S, D], F32), "mem": ([256, D], F32), "gains": ([128, (1 + 6 * L) * DC], F32),
        "w_in": ([L, D, NPROJ], F32), "w_out": ([L, D, D], F32), "wq": ([L, D, 1024], F32),
        "wk": ([L, D, 1024], F32), "wv": ([L, D, 1024], F32), "wo": ([L, 1024, D], F32),
        "wg": ([L, D, FFN], F32), "wu": ([L, D, FFN], F32), "wd": ([L, FFN, D], F32),
        "retg": ([128, L * 16], F32), "hgg": ([128, L * 8], F32), "lbl": ([128, L * 16], F32),
        "fw": ([L, 8, 128, 128], F32), "fb": ([128, L * 8], F32),
        "c_idf": ([128, 128], F32), "c_idb": ([128, 128], BF16), "c_onesb": ([128, 128], BF16),
        "c_onesf": ([128, 128], F32),
        "c_rot": ([128, 2 * S], F32), "c_dec": ([128, 8 * 6 * 512], F32),
        "c_tri": ([128, 642], F32), "c_cs": ([128, 256], BF16), "c_dft": ([2, S, S], BF16),
    }
    decl = {}

    def I(name):
        if name not in decl:
            shape, dt = specs[name]
            decl[name] = nc.dram_tensor(name, list(shape), dt, kind="ExternalInput").ap()
        return decl[name]

    out = nc.dram_tensor("out", [S, D], F32, kind="ExternalOutput").ap()

    def dscr(name, shape, dt):
        kind = "ExternalOutput" if debug else "Internal"
        return Buf(nc.dram_tensor(name, list(shape), dt, kind=kind).ap(), multi=True)

    hT = dscr("hT", [D, S], F32)
    zT = dscr("zT", [D, S], BF16)
    yT = dscr("yT", [D, S], F32)
    projT = dscr("projT", [NPROJ, S], F32)
    mixT = dscr("mixT", [D, S], BF16)
    memT = dscr("memT", [D, 256], BF16)
    kTms = [dscr(f"kTm{l}", [1024, 256], F32) for l in range(L)]
    vTms = [dscr(f"vTm{l}", [1024, 256], F32) for l in range(L)]
    qT = dscr("qT", [1024, S], F32)
    aoT = dscr("aoT", [1024, S], BF16)
    actT = dscr("actT", [FFN, S], BF16)
    gsT = dscr("gsT", [FFN, S], BF16)
    memT32 = dscr("memT32", [D, 256], F32)
    xin = Buf(I('x'), multi=True)
    outb = Buf(out, multi=True)

    mm = lambda o, l, r, st, sp: (lambda e: e.matmul(o, l, r, start=st, stop=sp))
    dbg_n = [0]
    dbg_seen = set()

    def dbg(name, b, ap, shape, dt):
        if not debug:
            return
        dbg_n[0] += 1
        if ("dbg_" + name) in dbg_seen:
            return
        dbg_seen.add("dbg_" + name)
        d = nc.dram_tensor("dbg_" + name, list(shape), dt, kind="ExternalOutput").ap()
        kb.dma('sp', 'dbg', d, ap, reads=[b], writes=[])

    def load_consts(sg):
        c = {}
        for name, ap, shape, dt in (("idf", I('c_idf'), [128, 128], F32), ("idb", I('c_idb'), [128, 128], BF16),
                                    ("onesb", I('c_onesb'), [128, 128], BF16), ("onesf", I('c_onesf'), [128, 128], F32)):
            b = sg.sb("k_" + name, shape, dt)
            kb.dma('sp', 'const', b[:], ap[:, :], writes=[b])
            c[name] = b
        return c

    def transpose_in(src, src_rows, dstT, sg_name):
        G = min(512, src_rows)
        TT = G // 128
        ng = src_rows // G
        with Stage(kb) as sg:
            c = load_consts(sg)
            xb = sg.sb("xt", [128, TT, D], F32)
            ob = sg.sb("ot", [128, DC, G], F32)
            pp = [sg.ps(f"pp{i}", [128, G]) for i in range(4)]
            for g in range(ng):
                kb.dma('sp', 'xt', xb[:], src.t[g * G:(g + 1) * G, :].rearrange("(t p) d -> p t d", p=128),
                       reads=[src], writes=[xb])
                for cc in range(DC):
                    p = pp[cc % 4]
                    for t in range(TT):
                        kb.op('pe', lambda e, p=p, t=t, cc=cc: e.transpose(
                            p[:, t * 128:(t + 1) * 128], xb[:, t, cc * 128:(cc + 1) * 128], c["idf"][:]),
                            reads=[xb, c["idf"]], writes=[p])
                    if cc % 2:
                        kb.op('act', lambda e, p=p, cc=cc: e.copy(ob[:, cc, :], p[:]), reads=[p], writes=[ob])
                    else:
                        kb.op('dve', lambda e, p=p, cc=cc: e.tensor_copy(ob[:, cc, :], p[:]), reads=[p], writes=[ob])
                kb.dma('sp', 'ot', dstT.t[:, g * G:(g + 1) * G].rearrange("(c p) t -> p c t", p=128), ob[:],
                       reads=[ob], writes=[dstT])

    def transpose_out():
        with Stage(kb) as sg:
            c = load_consts(sg)
            ht = sg.sb("ht", [128, DC, 512], F32)
            ob = sg.sb("ob", [128, 4, D], F32)
            pp = [sg.ps(f"pp{i}", [128, 512]) for i in range(4)]
            for g in range(NG):
                kb.dma('sp', 'ht', ht[:], hT.t[:, g * 512:(g + 1) * 512].rearrange("(c p) t -> p c t", p=128),
                       reads=[hT], writes=[ht])
                for t in range(4):
                    for c4 in range(DC // 4):
                        p = pp[c4 % 4]
                        for j in range(4):
                            cc = c4 * 4 + j
                            kb.op('pe', lambda e, p=p, t=t, cc=cc, j=j: e.transpose(
                                p[:, j * 128:(j + 1) * 128], ht[:, cc, t * 128:(t + 1) * 128], c["idf"][:]),
                                reads=[ht, c["idf"]], writes=[p])
                        if c4 % 2:
                            kb.op('act', lambda e, p=p, t=t, c4=c4: e.copy(ob[:, t, c4 * 512:(c4 + 1) * 512], p[:]),
                                  reads=[p], writes=[ob])
                        else:
                            kb.op('dve', lambda e, p=p, t=t, c4=c4: e.tensor_copy(ob[:, t, c4 * 512:(c4 + 1) * 512], p[:]),
                                  reads=[p], writes=[ob])
                kb.dma('sp', 'ob', out[g * 512:(g + 1) * 512, :].rearrange("(t p) d -> p t d", p=128), ob[:],
                       reads=[ob], writes=[outb])

    def norm_stage(srcT, dst, ntok, gslot, ysrc=None, gpost=None, hdst=None):
        TG = 256
        with Stage(kb) as sg:
            c = load_consts(sg)
            gt = sg.sb("gt", [128, (1 + 6 * L) * DC], F32)
            kb.dma('sp', 'const', gt[:], I('gains')[:, :], writes=[gt])
            h = sg.sb("h", [128, DC, TG], F32)
            y = sg.sb("y", [128, DC, TG], F32) if ysrc is not None else None
            sq = sg.sb("sq", [128, DC, TG], BF16)
            z = sg.sb("z", [128, DC, TG], BF16)
            rs = sg.sb("rs", [128, TG], F32)
            ps = sg.ps("ps", [128, TG])

            def stats(src_b):
                for cc in range(DC):
                    kb.op('act', lambda e, cc=cc: e.activation(sq[:, cc, :], src_b[:, cc, :], AF.Square),
                          reads=[src_b], writes=[sq])
                for cc in range(DC):
                    kb.op('pe', mm(ps[:], c["onesb"][:], sq[:, cc, :], cc == 0, cc == DC - 1),
                          reads=[sq, c["onesb"]], writes=[ps])
                kb.op('dve', lambda e: e.tensor_scalar(rs[:], ps[:], 1.0 / D, RMS_EPS, ALU.mult, ALU.add),
                      reads=[ps], writes=[rs])
                kb.op('act', lambda e: e.sqrt(rs[:], rs[:]), reads=[rs], writes=[rs])
                kb.op('dve', lambda e: e.reciprocal(rs[:], rs[:]), reads=[rs], writes=[rs])

            for g in range(ntok // TG):
                sl = slice(g * TG, (g + 1) * TG)
                kb.dma('sp', 'nh', h[:], srcT.t[:, sl].rearrange("(c p) t -> p c t", p=128), reads=[srcT], writes=[h])
                if ysrc is not None:
                    kb.dma('sp', 'ny', y[:], ysrc.t[:, sl].rearrange("(c p) t -> p c t", p=128), reads=[ysrc], writes=[y])
                    stats(y)
                    for cc in range(DC):
                        gi = gpost * DC + cc
                        kb.op('dve', lambda e, cc=cc, gi=gi: e.scalar_tensor_tensor(
                            y[:, cc, :], y[:, cc, :], gt[:, gi:gi + 1], rs[:], ALU.mult, ALU.mult),
                            reads=[y, gt, rs], writes=[y])
                        kb.op('pool', lambda e, cc=cc: e.tensor_tensor(h[:, cc, :], h[:, cc, :], y[:, cc, :], ALU.add),
                              reads=[h, y], writes=[h])
                    kb.dma('sp', 'nho', hdst.t[:, sl].rearrange("(c p) t -> p c t", p=128), h[:], reads=[h], writes=[hdst])
                if dst is not None:
                    stats(h)
                    for cc in range(DC):
                        gi = gslot * DC + cc
                        kb.op('dve', lambda e, cc=cc, gi=gi: e.scalar_tensor_tensor(
                            z[:, cc, :], h[:, cc, :], gt[:, gi:gi + 1], rs[:], ALU.mult, ALU.mult),
                            reads=[h, gt, rs], writes=[z])
                    kb.dma('sp', 'nz', dst.t[:, sl].rearrange("(c p) t -> p c t", p=128), z[:], reads=[z], writes=[dst])

    def linear_stage(srcT, K, ntok, jobs, TG=1024):
        KC = K // 128
        KS = KC if KC <= 32 else 43
        NSEG = KC // KS
        assert NSEG * KS == KC
        TG = min(TG, ntok)
        NH = max(1, TG // 512)
        HW = min(512, TG)
        with Stage(kb) as sg:
            zt = sg.sb("zt", [128, KC, TG], BF16)
            wst = [sg.sb(f"wst{i}", [128, KS, 128], F32) for i in range(2)]
            wbf = [sg.sb(f"wbf{i}", [128, KS, 128], BF16) for i in range(2)]
            pacc = [[sg.ps(f"pa{i}_{hh}", [128, HW]) for hh in range(NH)] for i in range(2)]
            ctx = {"sg": sg}
            for job in jobs:
                if job.get("setup"):
                    job["setup"](sg, ctx)
            wi = 0
            pi = 0
            for g in range(ntok // TG):
                kb.dma('sp', 'lz', zt[:], srcT.t[:, g * TG:(g + 1) * TG].rearrange("(c p) t -> p c t", p=128),
                       reads=[srcT], writes=[zt])
                for job in jobs:
                    W, N, evac = job["W"], job["N"], job["evac"]
                    for oc in range(N // 128):
                        pa = pacc[pi % 2]
                        pi += 1
                        for sgi in range(NSEG):
                            ws, wb = wst[wi % 2], wbf[wi % 2]
                            kb.dma('sp', f'lw{wi % 2}', ws[:],
                                   W[sgi * KS * 128:(sgi + 1) * KS * 128, oc * 128:(oc + 1) * 128].rearrange("(c p) n -> p c n", p=128),
                                   writes=[ws])
                            wi += 1
                            kb.op('pool', lambda e, ws=ws, wb=wb: e.tensor_copy(wb[:], ws[:]), reads=[ws], writes=[wb])
                            for hh in range(NH):
                                for kc in range(KS):
                                    kk = sgi * KS + kc
                                    kb.op('pe', mm(pa[hh][:], wb[:, kc, :], zt[:, kk, hh * HW:(hh + 1) * HW], kk == 0, kk == KC - 1),
                                          reads=[wb, zt], writes=[pa[hh]])
                        evac(oc, g * TG, pa, HW, ctx)

    def evac_to_dram(dst, row0, dt, key):
        def setup(sg, ctx):
            ctx[key] = [sg.sb(f"ev_{key}{i}", [128, 1024], dt) for i in range(2)]
            ctx[key + "_i"] = 0

        def evac(oc, t0, pa, HW, ctx):
            i = ctx[key + "_i"]
            ctx[key + "_i"] += 1
            ob = ctx[key][i % 2]
            for hh, p in enumerate(pa):
                if (i + hh) % 2:
                    kb.op('act', lambda e, p=p, hh=hh: e.copy(ob[:, hh * HW:(hh + 1) * HW], p[:]), reads=[p], writes=[ob])
                else:
                    kb.op('dve', lambda e, p=p, hh=hh: e.tensor_copy(ob[:, hh * HW:(hh + 1) * HW], p[:]), reads=[p], writes=[ob])
            n = HW * len(pa)
            kb.dma('sp', f'ev{key}{i % 2}', dst.t[row0 + oc * 128: row0 + (oc + 1) * 128, t0:t0 + n], ob[:, 0:n],
                   reads=[ob], writes=[dst])
        return setup, evac

    def simple_linear(srcT, K, ntok, W, N, dst, row0=0, dt=F32, TG=1024):
        setup, evac = evac_to_dram(dst, row0, dt, "o")
        linear_stage(srcT, K, ntok, [{"W": W, "N": N, "evac": evac, "setup": setup}], TG=TG)


    def gidx(l, slot):
        return 0 if slot == G_MEM else 1 + l * 6 + (slot - 1)

    def rsqrt_inplace(t, scale, eps, src):
        kb.op('dve', lambda e: e.tensor_scalar(t[:], src[:], scale, eps, ALU.mult, ALU.add), reads=[src], writes=[t])
        kb.op('act', lambda e: e.sqrt(t[:], t[:]), reads=[t], writes=[t])
        kb.op('dve', lambda e: e.reciprocal(t[:], t[:]), reads=[t], writes=[t])

    def retention_stage(l):
        gf = 1.0 - np.power(2.0, -5.0 - np.arange(8, dtype=np.float64))
        gb = gf[::-1]
        with Stage(kb) as sg:
            c = load_consts(sg)
            rot = sg.sb("rot", [128, 2 * S], F32)
            kb.dma('sp', 'const', rot[:], I('c_rot')[:, :], writes=[rot])
            rg_ = sg.sb("retg", [128, L * 16], F32)
            kb.dma('sp', 'const', rg_[:], I('retg')[:, :], writes=[rg_])
            dec = sg.sb("dec", [128, 6 * 512], F32)
            ld = sg.sb("ld", [128, 2, S], F32)
            qr = sg.sb("qr", [128, 2, S], BF16)
            kr = sg.sb("kr", [128, 2, S], BF16)
            t1 = sg.sb("t1", [128, S], F32)
            t2 = sg.sb("t2", [128, S], F32)
            vb = sg.sb("vb", [128, NT, 256], BF16)
            gt = sg.sb("gt", [128, 2, 512], F32)
            of = sg.sb("of", [128, 2, 512], F32)
            osq = sg.sb("osq", [128, 2, 512], F32)
            mu = sg.sb("mu", [128, 512], F32)
            var = sg.sb("var", [128, 512], F32)
            ob = sg.sb("ob", [128, 2, 512], BF16)
            pts = [sg.sb(f"pt{i}", [128, 512], BF16) for i in range(3)]
            sc = [sg.ps(f"sc{i}", [128, 512]) for i in range(2)]
            oacc = [sg.ps(f"oacc{i}", [128, 512]) for i in range(2)]
            ptr = sg.ps("ptr", [128, 256])
            ps1 = sg.ps("ps1", [128, 512])
            ps2 = sg.ps("ps2", [128, 512])
            cos, sin = (lambda: rot[:, 0:S]), (lambda: rot[:, S:2 * S])
            import os as _os
            RS = int(_os.environ.get("RSTOP", "9"))
            for h in range(int(_os.environ.get("RHEADS", "8"))):
                kb.dma('sp', 'rdec', dec[:], I('c_dec')[:, h * 3072:(h + 1) * 3072], writes=[dec])
                for (row0, dstb) in ((O_RQ, qr), (O_RK, kr)):
                    kb.dma('sp', 'rld', ld[:], projT.t[row0 + h * 256: row0 + (h + 1) * 256, :].rearrange("(c p) t -> p c t", p=128),
                           reads=[projT], writes=[ld])
                    kb.op('dve', lambda e: e.tensor_tensor(t1[:], ld[:, 0, :], cos(), ALU.mult), reads=[ld, rot], writes=[t1])
                    kb.op('dve', lambda e: e.tensor_tensor(t2[:], ld[:, 1, :], sin(), ALU.mult), reads=[ld, rot], writes=[t2])
                    kb.op('dve', lambda e, dstb=dstb: e.tensor_tensor(dstb[:, 0, :], t1[:], t2[:], ALU.subtract), reads=[t1, t2], writes=[dstb])
                    kb.op('dve', lambda e: e.tensor_tensor(t1[:], ld[:, 0, :], sin(), ALU.mult), reads=[ld, rot], writes=[t1])
                    kb.op('dve', lambda e: e.tensor_tensor(t2[:], ld[:, 1, :], cos(), ALU.mult), reads=[ld, rot], writes=[t2])
                    kb.op('dve', lambda e, dstb=dstb: e.tensor_tensor(dstb[:, 1, :], t1[:], t2[:], ALU.add), reads=[t1, t2], writes=[dstb])
                if RS <= 1:
                    continue
                kb.dma('sp', 'rld', ld[:], projT.t[O_RV + h * 256: O_RV + (h + 1) * 256, :].rearrange("(c p) t -> p c t", p=128),
                       reads=[projT], writes=[ld])
                for mt in range(NT):
                    for ec in range(2):
                        kb.op('pe', lambda e, mt=mt, ec=ec: e.transpose(ptr[:, ec * 128:(ec + 1) * 128], ld[:, ec, mt * 128:(mt + 1) * 128], c["idf"][:]),
                              reads=[ld, c["idf"]], writes=[ptr])
                    kb.op('act', lambda e, mt=mt: e.copy(vb[:, mt, :], ptr[:]), reads=[ptr], writes=[vb])
                if RS <= 2:
                    continue
                for ng in range(NG):
                    kb.dma('sp', 'rgt', gt[:], projT.t[O_RG + h * 256: O_RG + (h + 1) * 256, ng * 512:(ng + 1) * 512].rearrange("(c p) t -> p c t", p=128),
                           reads=[projT], writes=[gt])
                    kb.op('act', lambda e: e.activation(gt[:], gt[:], AF.Silu), reads=[gt], writes=[gt])
                    for mb in range(NT):
                        scp = sc[mb % 2]
                        for dc in range(2):
                            kb.op('pe', mm(scp[:], kr[:, dc, mb * 128:(mb + 1) * 128], qr[:, dc, ng * 512:(ng + 1) * 512], dc == 0, dc == 1),
                                  reads=[kr, qr], writes=[scp])
                        d0 = 512 * ng - 128 * mb
                        if d0 >= 128:
                            tb, scal = 0, float(gf[h] ** d0) / 16.0
                        elif d0 <= -512:
                            tb, scal = 1, float(gb[h] ** (-d0)) / 16.0
                        else:
                            tb, scal = 2 + (-d0) // 128, 1.0 / 16.0
                        pt = pts[mb % 3]
                        kb.op('dve', lambda e, scp=scp, pt=pt, tb=tb, scal=scal: e.scalar_tensor_tensor(
                            pt[:], scp[:], scal, dec[:, tb * 512:(tb + 1) * 512], ALU.mult, ALU.mult),
                            reads=[scp, dec], writes=[pt])
                        for ec in range(2):
                            kb.op('pe', mm(oacc[ec][:], vb[:, mb, ec * 128:(ec + 1) * 128], pt[:], mb == 0, mb == NT - 1),
                                  reads=[vb, pt], writes=[oacc[ec]])
                    for ec in range(2):
                        kb.op('act', lambda e, ec=ec: e.copy(of[:, ec, :], oacc[ec][:]), reads=[oacc[ec]], writes=[of])
                        kb.op('act', lambda e, ec=ec: e.activation(osq[:, ec, :], oacc[ec][:], AF.Square), reads=[oacc[ec]], writes=[osq])
                    for ec in range(2):
                        kb.op('pe', mm(ps1[:], c["onesf"][:], of[:, ec, :], ec == 0, ec == 1), reads=[of, c["onesf"]], writes=[ps1])
                    for ec in range(2):
                        kb.op('pe', mm(ps2[:], c["onesf"][:], osq[:, ec, :], ec == 0, ec == 1), reads=[osq, c["onesf"]], writes=[ps2])
                    kb.op('dve', lambda e: e.tensor_scalar(mu[:], ps1[:], 1.0 / 256, None, ALU.mult), reads=[ps1], writes=[mu])
                    kb.op('dve', lambda e: e.tensor_tensor(var[:], mu[:], mu[:], ALU.mult), reads=[mu], writes=[var])
                    kb.op('dve', lambda e: e.scalar_tensor_tensor(var[:], ps2[:], 1.0 / 256, var[:], ALU.mult, ALU.subtract),
                          reads=[ps2, var], writes=[var])
                    rsqrt_inplace(var, 1.0, GN_EPS, var)
                    for ec in range(2):
                        gi = l * 16 + h * 2 + ec
                        kb.op('dve', lambda e, ec=ec: e.tensor_tensor(of[:, ec, :], of[:, ec, :], mu[:], ALU.subtract), reads=[of, mu], writes=[of])
                        kb.op('dve', lambda e, ec=ec: e.tensor_tensor(of[:, ec, :], of[:, ec, :], var[:], ALU.mult), reads=[of, var], writes=[of])
                        kb.op('dve', lambda e, ec=ec, gi=gi: e.scalar_tensor_tensor(
                            ob[:, ec, :], of[:, ec, :], rg_[:, gi:gi + 1], gt[:, ec, :], ALU.mult, ALU.mult),
                            reads=[of, rg_, gt], writes=[ob])
                    kb.dma('sp', 'rob', mixT.t[h * 256:(h + 1) * 256, ng * 512:(ng + 1) * 512].rearrange("(c p) t -> p c t", p=128), ob[:],
                           reads=[ob], writes=[mixT])

    def hgrn_stage(l):
        NCH = 2 * NT
        with Stage(kb) as sg:
            c = load_consts(sg)
            tri = sg.sb("tri", [128, 642], F32)
            kb.dma('sp', 'const', tri[:], I('c_tri')[:, :], writes=[tri])
            hg_ = sg.sb("hgg", [128, L * 8], F32)
            kb.dma('sp', 'const', hg_[:], I('hgg')[:, :], writes=[hg_])
            lb = sg.sb("lb", [128, 16], F32)
            oml = sg.sb("oml", [128, 16], F32)
            if l == 0:
                kb.op('dve', lambda e: e.tensor_scalar(lb[:], tri[:, 0:16], 0.0, None, ALU.mult), reads=[tri], writes=[lb])
            else:
                l0 = sg.sb("l0", [128, 16], F32)
                kb.dma('sp', 'const', lb[:], I('lbl')[:, 16:32], writes=[lb])
                kb.dma('sp', 'const', l0[:], I('lbl')[:, 0:16], writes=[l0])
                kb.op('dve', lambda e: e.tensor_tensor(lb[:], lb[:], l0[:], ALU.subtract), reads=[lb, l0], writes=[lb])
                kb.op('act', lambda e: e.activation(lb[:], lb[:], AF.Sigmoid), reads=[lb], writes=[lb])
            kb.op('dve', lambda e: e.tensor_scalar(oml[:], lb[:], -1.0, 1.0, ALU.mult, ALU.add), reads=[lb], writes=[oml])
            qTs = sg.sb("qTs", [128, S], F32)
            zb = sg.sb("zb", [128, S], F32)
            lf = sg.sb("lf", [128, S], F32)
            qt = [sg.sb(f"qt{d}", [128, S], BF16) for d in range(2)]
            AT = sg.sb("AT", [128, S], F32)
            ATb = sg.sb("ATb", [128, S], BF16)
            vb = sg.sb("vb", [128, NT, 128], BF16)
            vbm = [sg.sb(f"vbm{i}", [128, NT, 128], BF16) for i in range(2)]
            dse = sg.sb("dse", [128, NCH, 128], F32)
            sbf = [sg.sb(f"sbf{d}", [128, NCH, 128], BF16) for d in range(2)]
            elast = sg.sb("elast", [128, NCH], F32)
            run = sg.sb("run", [128, 128], F32)
            lft = sg.sb("lft", [128, 128], F32)
            ecum = sg.sb("ecum", [128, 128], F32)
            encum = sg.sb("encum", [128, 128], F32)
            ktT = sg.sb("ktT", [128, 128], BF16)
            ktk = sg.sb("ktk", [128, 128], BF16)
            gtile = sg.sb("gtile", [128, 512], F32)
            osq = sg.sb("osq", [128, 512], F32)
            rstd = sg.sb("rstd", [128, 512], F32)
            ob = sg.sb("ob", [128, 512], BF16)
            bankA = sg.ps("bankA", [128, 512])
            p_tr = Buf(bankA.t[:, 0:128])
            p_cum = Buf(bankA.t[:, 128:256])
            p_sc = Buf(bankA.t[:, 256:384])
            p_last = Buf(bankA.t[:, 384:386])
            p_kt = sg.ps("p_kt", [128, 128], BF16)
            bankB = sg.ps("bankB", [128, 256])
            p_ds = [Buf(bankB.t[:, i * 128:(i + 1) * 128]) for i in range(2)]
            p_o = sg.ps("p_o", [128, 512])
            p_ss = sg.ps("p_ss", [128, 512])
            import os as _os
            HS = int(_os.environ.get("HSTOP", "9"))
            for h in range(int(_os.environ.get("HHEADS", "8"))):
                if HS <= 0:
                    continue
                rq = O_HQ + h * 128
                kb.dma('sp', 'hq', qTs[:], projT.t[rq:rq + 128, :], reads=[projT], writes=[qTs])
                ri = O_HI + h * 128
                kb.dma('sp', 'hz', zb[:], projT.t[ri:ri + 128, :], reads=[projT], writes=[zb])
                for j in range(NT):
                    kb.op('pe', lambda e, j=j: e.transpose(p_tr[:], zb[:, j * 128:(j + 1) * 128], c["idf"][:]),
                          reads=[zb, c["idf"]], writes=[p_tr])
                    kb.op('act', lambda e, j=j: e.copy(vb[:, j, :], p_tr[:]), reads=[p_tr], writes=[vb])
                    for cc in range(2):
                        kb.op('pool', lambda e, j=j, cc=cc: e.tensor_tensor(vbm[cc][:, j, :], vb[:, j, :], tri[:, 258 + cc * 128:258 + (cc + 1) * 128], ALU.mult),
                              reads=[vb, tri], writes=[vbm[cc]])
                for di in range(2):
                    if HS <= 1:
                        continue
                    rz = (O_HFF if di == 0 else O_HFB) + h * 128
                    li = di * 8 + h
                    kb.dma('sp', 'hz', zb[:], projT.t[rz:rz + 128, :], reads=[projT], writes=[zb])
                    kb.op('act', lambda e: e.activation(zb[:], zb[:], AF.Sigmoid), reads=[zb], writes=[zb])
                    if h == 0 and di == 0:
                        dbg("sig", zb, zb[:], [128, S], F32)
                        dbg("lb", lb, lb[:], [128, 16], F32)
                        dbg("oml", oml, oml[:], [128, 16], F32)
                    kb.op('act', lambda e, li=li: e.activation(zb[:], zb[:], AF.Identity, bias=lb[:, li:li + 1], scale=oml[:, li:li + 1]),
                          reads=[zb, oml, lb], writes=[zb])
                    if h == 0 and di == 0:
                        dbg("f", zb, zb[:], [128, S], F32)
                    kb.op('act', lambda e: e.activation(lf[:], zb[:], AF.Ln), reads=[zb], writes=[lf])
                    kb.op('dve', lambda e: e.tensor_scalar(zb[:], zb[:], -1.0, 1.0, ALU.mult, ALU.add), reads=[zb], writes=[zb])
                    if HS <= 2:
                        continue
                    for j in range(NT):
                        ts_ = slice(j * 128, (j + 1) * 128)
                        kb.op('pe', lambda e, ts_=ts_: e.transpose(p_tr[:], lf[:, ts_], c["idf"][:]), reads=[lf, c["idf"]], writes=[p_tr])
                        kb.op('act', lambda e: e.copy(lft[:], p_tr[:]), reads=[p_tr], writes=[lft])
                        kb.op('pe', mm(p_cum[:], lft[:], tri[:, di * 128:(di + 1) * 128], True, True), reads=[lft, tri], writes=[p_cum])
                        for cc in range(2):
                            col = (cc * 64 + 63) if di == 0 else (cc * 64)
                            kb.op('act', lambda e, j=j, cc=cc, col=col: e.activation(
                                elast[:, 2 * j + cc:2 * j + cc + 1], p_cum[:, col:col + 1], AF.Exp), reads=[p_cum], writes=[elast])
                        kb.op('act', lambda e: e.activation(ecum[:], p_cum[:], AF.Exp), reads=[p_cum], writes=[ecum])
                        kb.op('act', lambda e: e.activation(encum[:], p_cum[:], AF.Exp, scale=-1.0), reads=[p_cum], writes=[encum])
                        kb.op('dve', lambda e, ts_=ts_, di=di: e.scalar_tensor_tensor(
                            qt[di][:, ts_], qTs[:, ts_], float(128 ** -0.5), ecum[:], ALU.mult, ALU.mult),
                            reads=[qTs, ecum], writes=[qt[di]])
                        if HS <= 3:
                            continue
                        kb.op('dve', lambda e, ts_=ts_: e.tensor_tensor(ktT[:], zb[:, ts_], encum[:], ALU.mult), reads=[zb, encum], writes=[ktT])
                        kb.op('pe', lambda e: e.transpose(p_kt[:], ktT[:], c["idb"][:]), reads=[ktT, c["idb"]], writes=[p_kt])
                        kb.op('act', lambda e: e.copy(ktk[:], p_kt[:]), reads=[p_kt], writes=[ktk])
                        kb.op('pe', mm(p_sc[:], ktT[:], qt[di][:, ts_], True, True), reads=[ktT, qt[di]], writes=[p_sc])
                        if di == 0:
                            kb.op('dve', lambda e, ts_=ts_: e.tensor_tensor(AT[:, ts_], p_sc[:], tri[:, 0:128], ALU.mult),
                                  reads=[p_sc, tri], writes=[AT])
                        else:
                            kb.op('dve', lambda e: e.tensor_tensor(ecum[:], p_sc[:], tri[:, 128:256], ALU.mult),
                                  reads=[p_sc, tri], writes=[ecum])
                            kb.op('dve', lambda e, ts_=ts_: e.tensor_tensor(ATb[:, ts_], AT[:, ts_], ecum[:], ALU.add),
                                  reads=[AT, ecum], writes=[ATb])
                        for cc in range(2):
                            ch = 2 * j + cc
                            pd = p_ds[cc]
                            kb.op('pe', mm(pd[:], ktk[:], vbm[cc][:, j, :], True, True),
                                  reads=[ktk, vbm[cc]], writes=[pd])
                            kb.op('act', lambda e, pd=pd, ch=ch: e.copy(dse[:, ch, :], pd[:]), reads=[pd], writes=[dse])
                    kb.op('dve', lambda e: e.tensor_scalar(run[:], tri[:, 0:128], 0.0, None, ALU.mult), reads=[tri], writes=[run])
                    order = range(NCH) if di == 0 else range(NCH - 1, -1, -1)
                    for ch in order:
                        kb.op('dve', lambda e, ch=ch, di=di: e.tensor_copy(sbf[di][:, ch, :], run[:]), reads=[run], writes=[sbf[di]])
                        kb.op('dve', lambda e, ch=ch: e.tensor_tensor(run[:], run[:], dse[:, ch, :], ALU.add),
                              reads=[run, dse], writes=[run])
                        kb.op('dve', lambda e, ch=ch: e.scalar_tensor_tensor(
                            run[:], run[:], elast[:, ch:ch + 1], tri[:, 514:642], ALU.mult, ALU.add),
                            reads=[run, elast, tri], writes=[run])
                rgt = O_HG + h * 128
                if h == 0:
                    dbg("lf", lf, lf[:], [128, S], F32)
                    dbg("k", zb, zb[:], [128, S], F32)
                    dbg("qt0", qt[0], qt[0][:], [128, S], BF16)
                    dbg("qt1", qt[1], qt[1][:], [128, S], BF16)
                    dbg("AT", AT, AT[:], [128, S], F32)
                    dbg("ATb", ATb, ATb[:], [128, S], BF16)
                    dbg("elast", elast, elast[:], [128, NCH], F32)
                    dbg("dse", dse, dse[:], [128, NCH, 128], F32)
                    dbg("sbf0", sbf[0], sbf[0][:], [128, NCH, 128], BF16)
                    dbg("sbf1", sbf[1], sbf[1][:], [128, NCH, 128], BF16)
                    dbg("vb", vb, vb[:], [128, NT, 128], BF16)
                if HS <= 4:
                    continue
                for ng in range(NG):
                    for jj in range(4):
                        j = ng * 4 + jj
                        kb.op('pe', mm(p_o[:, jj * 128:(jj + 1) * 128], vb[:, j, :], ATb[:, j * 128:(j + 1) * 128], True, False),
                              reads=[vb, ATb], writes=[p_o])
                        for di in range(2):
                            for cc in range(2):
                                ch = 2 * j + cc
                                cols = slice(jj * 128 + cc * 64, jj * 128 + (cc + 1) * 64)
                                kb.op('pe', mm(p_o[:, cols], sbf[di][:, ch, :], qt[di][:, ch * 64:(ch + 1) * 64], False, di == 1 and cc == 1),
                                      reads=[sbf[di], qt[di]], writes=[p_o])
                    kb.dma('sp', 'hg', gtile[:], projT.t[rgt:rgt + 128, ng * 512:(ng + 1) * 512], reads=[projT], writes=[gtile])
                    kb.op('act', lambda e: e.activation(gtile[:], gtile[:], AF.Silu), reads=[gtile], writes=[gtile])
                    kb.op('act', lambda e: e.activation(osq[:], p_o[:], AF.Square), reads=[p_o], writes=[osq])
                    kb.op('pe', mm(p_ss[:], c["onesf"][:], osq[:], True, True), reads=[osq, c["onesf"]], writes=[p_ss])
                    rsqrt_inplace(rstd, 1.0 / 128, RMS_EPS, p_ss)
                    kb.op('dve', lambda e: e.tensor_tensor(osq[:], p_o[:], rstd[:], ALU.mult), reads=[p_o, rstd], writes=[osq])
                    gi = l * 8 + h
                    kb.op('dve', lambda e, gi=gi: e.scalar_tensor_tensor(ob[:], osq[:], hg_[:, gi:gi + 1], gtile[:], ALU.mult, ALU.mult),
                          reads=[osq, hg_, gtile], writes=[ob])
                    kb.dma('sp', 'hob', mixT.t[2048 + h * 128: 2048 + (h + 1) * 128, ng * 512:(ng + 1) * 512], ob[:],
                           reads=[ob], writes=[mixT])


    def fft_stage(l):
        GP = 4
        with Stage(kb) as sg:
            cs = sg.sb("cs", [128, 256], BF16)
            kb.dma('sp', 'const', cs[:], I('c_cs')[:, :], writes=[cs])
            fb_ = sg.sb("fb", [128, L * 8], F32)
            kb.dma('sp', 'const', fb_[:], I('fb')[:, :], writes=[fb_])
            zf = sg.sb("zf", [128, S], F32)
            zbf = sg.sb("zbf", [128, S], BF16)
            UV = [sg.sb(f"UV{i}", [128, NT, 256], BF16) for i in range(GP)]
            fwf = sg.sb("fwf", [128, 128], F32)
            fwb = [sg.sb(f"fwb{i}", [128, 128], BF16) for i in range(GP)]
            dt_ = [sg.sb(f"dt{i}", [128, 2, 512], BF16) for i in range(3)]
            spec = sg.sb("spec", [128, 512], BF16)
            ob = sg.sb("ob", [128, 512], BF16)
            p_uv = sg.ps("p_uv", [128, 256])
            acc = [sg.ps(f"acc{i}", [128, 512]) for i in range(GP)]
            p_y = sg.ps("p_y", [128, 512])
            for g0 in range(0, 8, GP):
                for gi in range(GP):
                    g = g0 + gi
                    kb.dma('sp', 'fz', zf[:], projT.t[O_FZ + g * 128: O_FZ + (g + 1) * 128, :], reads=[projT], writes=[zf])
                    kb.op('pool', lambda e: e.tensor_copy(zbf[:], zf[:]), reads=[zf], writes=[zbf])
                    kb.dma('sp', 'fw', fwf[:], I('fw')[l, g, :, :], writes=[fwf])
                    kb.op('pool', lambda e, gi=gi: e.tensor_copy(fwb[gi][:], fwf[:]), reads=[fwf], writes=[fwb[gi]])
                    for pt in range(NT):
                        kb.op('pe', mm(p_uv[:], zbf[:, pt * 128:(pt + 1) * 128], cs[:], True, True), reads=[zbf, cs], writes=[p_uv])
                        kb.op('act', lambda e, gi=gi, pt=pt: e.copy(UV[gi][:, pt, :], p_uv[:]), reads=[p_uv], writes=[UV[gi]])
                k = 0
                for pg in range(NG):
                    for pt in range(NT):
                        d = dt_[k % 3]
                        k += 1
                        kb.dma('sp', f'fd{k % 3}', d[:], I('c_dft')[:, pt * 128:(pt + 1) * 128, pg * 512:(pg + 1) * 512].rearrange("k p n -> p k n"),
                               writes=[d])
                        for gi in range(GP):
                            kb.op('pe', mm(acc[gi][:], UV[gi][:, pt, 0:128], d[:, 0, :], pt == 0, False), reads=[UV[gi], d], writes=[acc[gi]])
                            kb.op('pe', mm(acc[gi][:], UV[gi][:, pt, 128:256], d[:, 1, :], False, pt == NT - 1), reads=[UV[gi], d], writes=[acc[gi]])
                    for gi in range(GP):
                        g = g0 + gi
                        kb.op('act', lambda e, gi=gi: e.copy(spec[:], acc[gi][:]), reads=[acc[gi]], writes=[spec])
                        kb.op('pe', mm(p_y[:], fwb[gi][:], spec[:], True, True), reads=[fwb[gi], spec], writes=[p_y])
                        bi = l * 8 + g
                        kb.op('act', lambda e, bi=bi: e.activation(ob[:], p_y[:], AF.Identity, bias=fb_[:, bi:bi + 1]),
                              reads=[p_y, fb_], writes=[ob])
                        kb.dma('sp', 'fob', mixT.t[3072 + g * 128: 3072 + (g + 1) * 128, pg * 512:(pg + 1) * 512], ob[:],
                               reads=[ob], writes=[mixT])

    def xattn_core(kTm, vTm):
        with Stage(kb) as sg:
            c = load_consts(sg)
            kf = sg.sb("kf", [128, 8, 256], F32)
            kbf = sg.sb("kbf", [128, 8, 256], BF16)
            vf = sg.sb("vf", [128, 8, 256], F32)
            vtok = sg.sb("vtok", [128, 2, 1024], BF16)
            qf = sg.sb("qf", [128, 8, 128], F32)
            qb = sg.sb("qb", [128, 8, 128], BF16)
            P = sg.sb("P", [128, 4, 256], F32)
            Pn = sg.sb("Pn", [128, 4, 256], BF16)
            PT = sg.sb("PT", [128, 4, 2, 128], BF16)
            mx = sg.sb("mx", [128, 4], F32)
            rs = sg.sb("rs", [128, 4], F32)
            ao = sg.sb("ao", [128, 8, 128], BF16)
            p_sc = sg.ps("p_sc", [128, 1024])
            p_t = sg.ps("p_t", [128, 128])
            p_pt = sg.ps("p_pt", [128, 128], BF16)
            p_oo = sg.ps("p_oo", [128, 1024])
            kb.dma('sp', 'xk', kf[:], kTm.t[:, :].rearrange("(c p) m -> p c m", p=128), reads=[kTm], writes=[kf])
            kb.op('dve', lambda e: e.tensor_copy(kbf[:], kf[:]), reads=[kf], writes=[kbf])
            kb.dma('sp', 'xv', vf[:], vTm.t[:, :].rearrange("(c p) m -> p c m", p=128), reads=[vTm], writes=[vf])
            for ec in range(8):
                for mt in range(2):
                    kb.op('pe', lambda e, ec=ec, mt=mt: e.transpose(p_t[:], vf[:, ec, mt * 128:(mt + 1) * 128], c["idf"][:]),
                          reads=[vf, c["idf"]], writes=[p_t])
                    kb.op('act', lambda e, ec=ec, mt=mt: e.copy(vtok[:, mt, ec * 128:(ec + 1) * 128], p_t[:]), reads=[p_t], writes=[vtok])
            for t in range(NT):
                kb.dma('sp', 'xq', qf[:], qT.t[:, t * 128:(t + 1) * 128].rearrange("(c p) t -> p c t", p=128), reads=[qT], writes=[qf])
                kb.op('pool', lambda e: e.tensor_copy(qb[:], qf[:]), reads=[qf], writes=[qb])
                for hh in range(4):
                    for dc in range(2):
                        kb.op('pe', mm(p_sc[:, hh * 256:(hh + 1) * 256], qb[:, hh * 2 + dc, :], kbf[:, hh * 2 + dc, :], dc == 0, dc == 1),
                              reads=[qb, kbf], writes=[p_sc])
                kb.op('dve', lambda e: e.tensor_reduce(mx[:], p_sc[:].rearrange("p (h m) -> p h m", h=4), AX.X, ALU.max),
                      reads=[p_sc], writes=[mx])
                kb.op('dve', lambda e: e.tensor_scalar(mx[:], mx[:], -1.0 / 16, None, ALU.mult), reads=[mx], writes=[mx])
                for hh in range(4):
                    kb.op('act', lambda e, hh=hh: e.activation(P[:, hh, :], p_sc[:, hh * 256:(hh + 1) * 256], AF.Exp,
                                                               bias=mx[:, hh:hh + 1], scale=1.0 / 16, accum_out=rs[:, hh:hh + 1]),
                          reads=[p_sc, mx], writes=[P, rs])
                kb.op('dve', lambda e: e.reciprocal(rs[:], rs[:]), reads=[rs], writes=[rs])
                for hh in range(4):
                    for q2 in range(2):
                        kb.op('dve', lambda e, hh=hh, q2=q2: e.scalar_tensor_tensor(
                            Pn[:, hh, q2 * 128:(q2 + 1) * 128], P[:, hh, q2 * 128:(q2 + 1) * 128], rs[:, hh:hh + 1], c["onesf"][:], ALU.mult, ALU.mult),
                            reads=[P, rs, c["onesf"]], writes=[Pn])
                    for mt in range(2):
                        kb.op('pe', lambda e, hh=hh, mt=mt: e.transpose(p_pt[:], Pn[:, hh, mt * 128:(mt + 1) * 128], c["idb"][:]),
                              reads=[Pn, c["idb"]], writes=[p_pt])
                        kb.op('act', lambda e, hh=hh, mt=mt: e.copy(PT[:, hh, mt, :], p_pt[:]), reads=[p_pt], writes=[PT])
                for hh in range(4):
                    for ec in range(2):
                        o8 = hh * 2 + ec
                        for mt in range(2):
                            kb.op('pe', mm(p_oo[:, o8 * 128:(o8 + 1) * 128], vtok[:, mt, o8 * 128:(o8 + 1) * 128], PT[:, hh, mt, :], mt == 0, mt == 1),
                                  reads=[vtok, PT], writes=[p_oo])
                kb.op('act', lambda e: e.copy(ao[:].rearrange("p c t -> p (c t)"), p_oo[:]), reads=[p_oo], writes=[ao])
                kb.dma('sp', 'xao', aoT.t[:, t * 128:(t + 1) * 128].rearrange("(c p) t -> p c t", p=128), ao[:], reads=[ao], writes=[aoT])

    def ffn_gate_up(l):
        def setup1(sg, ctx):
            ctx["g"] = [sg.sb(f"evg{i}", [128, 1024], BF16) for i in range(2)]
            ctx["gi"] = 0

        def evac1(oc, t0, pa, HW, ctx):
            i = ctx["gi"]
            ctx["gi"] += 1
            ob = ctx["g"][i % 2]
            for hh, p in enumerate(pa):
                kb.op('act', lambda e, p=p, hh=hh: e.activation(ob[:, hh * HW:(hh + 1) * HW], p[:], AF.Silu), reads=[p], writes=[ob])
            n = HW * len(pa)
            kb.dma('sp', f'evg{i % 2}', gsT.t[oc * 128:(oc + 1) * 128, t0:t0 + n], ob[:, 0:n], reads=[ob], writes=[gsT])
        linear_stage(zT, D, S, [{"W": I('wg')[l], "N": FFN, "evac": evac1, "setup": setup1}])

        def setup2(sg, ctx):
            ctx["u"] = [sg.sb(f"evu{i}", [128, 1024], BF16) for i in range(2)]
            ctx["gl"] = [sg.sb(f"evl{i}", [128, 1024], BF16) for i in range(2)]
            ctx["ui"] = 0

        def evac2(oc, t0, pa, HW, ctx):
            i = ctx["ui"]
            ctx["ui"] += 1
            ob, gl = ctx["u"][i % 2], ctx["gl"][i % 2]
            n = HW * len(pa)
            kb.dma('sp', f'evl{i % 2}', gl[:, 0:n], gsT.t[oc * 128:(oc + 1) * 128, t0:t0 + n], reads=[gsT], writes=[gl])
            for hh, p in enumerate(pa):
                kb.op('dve', lambda e, p=p, hh=hh: e.tensor_tensor(ob[:, hh * HW:(hh + 1) * HW], p[:], gl[:, hh * HW:(hh + 1) * HW], ALU.mult),
                      reads=[p, gl], writes=[ob])
            kb.dma('sp', f'evu{i % 2}', actT.t[oc * 128:(oc + 1) * 128, t0:t0 + n], ob[:, 0:n], reads=[ob], writes=[actT])
        linear_stage(zT, D, S, [{"W": I('wu')[l], "N": FFN, "evac": evac2, "setup": setup2}])


    memb = Buf(I('mem'), multi=True)
    transpose_in(xin, S, hT, "x")
    transpose_in(memb, 256, memT32, "m")
    norm_stage(memT32, memT, 256, gidx(0, G_MEM))
    norm_stage(hT, zT, S, gidx(0, G_PREMIX))
    for l in range(L):
        simple_linear(memT, D, 256, I('wk')[l], 1024, kTms[l])
        simple_linear(memT, D, 256, I('wv')[l], 1024, vTms[l])
    import os as _os
    _stopl = int(_os.environ.get("STOPL", "0"))
    for l in range(L):
        simple_linear(zT, D, S, I('w_in')[l], NPROJ, projT)
        if stop_after == "proj" and l == _stopl:
            break
        import os as _os
        _mx = _os.environ.get("MIXERS", "r,h,f").split(",")
        if "r" in _mx:
            retention_stage(l)
        if "h" in _mx:
            hgrn_stage(l)
        if "f" in _mx:
            fft_stage(l)
        if stop_after == "mix" and l == _stopl:
            break
        simple_linear(mixT, D, S, I('w_out')[l], D, yT)
        norm_stage(hT, zT, S, gidx(l, G_PREX), ysrc=yT, gpost=gidx(l, G_POSTMIX), hdst=hT)
        if stop_after == "postmix" and l == _stopl:
            break
        simple_linear(zT, D, S, I('wq')[l], 1024, qT)
        if stop_after == "q" and l == _stopl:
            break
        xattn_core(kTms[l], vTms[l])
        if stop_after == "xcore" and l == _stopl:
            break
        simple_linear(aoT, 1024, S, I('wo')[l], D, yT)
        if stop_after == "wo" and l == _stopl:
            break
        norm_stage(hT, zT, S, gidx(l, G_PREF), ysrc=yT, gpost=gidx(l, G_POSTX), hdst=hT)
        if stop_after == "postx" and l == _stopl:
            break
        ffn_gate_up(l)
        simple_linear(actT, FFN, S, I('wd')[l], D, yT, TG=512)
        last = (l == L - 1)
        norm_stage(hT, None if last else zT, S, None if last else gidx(l + 1, G_PREMIX), ysrc=yT, gpost=gidx(l, G_POSTF), hdst=hT)
    transpose_out()
    kb.barrier()
    return nc, kb


def prep_core_inputs(inp, b, S, L, consts):
    f = lambda k: np.asarray(inp[k], np.float32)
    m = {}
    m["x"] = np.ascontiguousarray(f("x")[b])
    m["mem"] = np.ascontiguousarray(f("mem")[b])
    gl = [_pm(f("mem_norm_g"))]
    for l in range(L):
        for k in ("pre_mix_g", "post_mix_g", "pre_xattn_g", "post_xattn_g", "pre_ffn_g", "post_ffn_g"):
            gl.append(_pm(f(k)[l]))
    m["gains"] = np.ascontiguousarray(np.concatenate(gl, 1))
    m["w_in"] = f("w_in")[:L]
    m["w_out"] = f("w_out")[:L]
    m["wq"] = f("xattn_wq")[:L]
    m["wk"] = f("xattn_wk")[:L]
    m["wv"] = f("xattn_wv")[:L]
    m["wo"] = f("xattn_wo")[:L]
    m["wg"] = f("ffn_w_gate")[:L]
    m["wu"] = f("ffn_w_up")[:L]
    m["wd"] = f("ffn_w_down")[:L]
    m["retg"] = np.ascontiguousarray(np.concatenate([f("ret_norm_g")[l].reshape(16, 128).T for l in range(L)], 1))
    m["hgg"] = np.ascontiguousarray(np.concatenate([f("hgrn_norm_g")[l].reshape(8, 128).T for l in range(L)], 1))
    m["lbl"] = np.ascontiguousarray(np.concatenate([f("hgrn_lb_logits")[l].reshape(16, 128).T for l in range(L)], 1))
    m["fw"] = f("fourier_w")[:L]
    m["fb"] = np.ascontiguousarray(np.concatenate([f("fourier_b")[l].T for l in range(L)], 1))
    m.update(consts)
    return m


def input_names(nc):
    return [a.memorylocations[0].name for a in nc.allocations
            if getattr(a, 'kind', None) == "ExternalInput" and a.memorylocations[0].name != 'partition_id']


def kernel(**inputs):
    B, S, L = 2, 4096, 2
    nc, kb = build_program(S, L)
    consts = make_consts(S)
    names = input_names(nc)
    import gc
    import jax
    in_maps = []
    for b in range(B):
        m = prep_core_inputs(inputs, b, S, L, consts)
        in_maps.append({n: m[n] for n in names})
    res = None
    for attempt in range(2):
        try:
            res = run_bass_kernel_spmd(nc, in_maps, core_ids=list(range(B)))
            break
        except Exception:
            if attempt == 1:
                raise
            gc.collect()
            jax.clear_caches()
    return np.stack([np.array(res.results[b]["out"], np.float32) for b in range(B)], 0)


def _pm(v):
    return np.ascontiguousarray(np.asarray(v, np.float32).reshape(-1, 128).T)


def make_consts(S):
    bf = ml_dtypes.bfloat16
    c = {}
    c["c_idf"] = np.eye(128, dtype=np.float32)
    c["c_idb"] = np.eye(128).astype(bf)
    c["c_onesb"] = np.ones((128, 128)).astype(bf)
    c["c_onesf"] = np.ones((128, 128), np.float32)
    half = 128
    inv = np.power(np.float32(10000.0), -np.arange(half, dtype=np.float32) / np.float32(half)).astype(np.float32)
    ang = (np.arange(S, dtype=np.float32)[None, :] * inv[:, None]).astype(np.float32)
    c["c_rot"] = np.concatenate([np.cos(ang.astype(np.float64)), np.sin(ang.astype(np.float64))], 1).astype(np.float32)
    gf = 1.0 - np.power(2.0, -5.0 - np.arange(8, dtype=np.float64))
    gb = gf[::-1]
    mmi = np.arange(128)[:, None].astype(np.float64)
    nni = np.arange(512)[None, :].astype(np.float64)
    dec = np.zeros((128, 8, 6, 512), np.float32)
    for h in range(8):
        lf, lb_ = np.log(gf[h]), np.log(gb[h])
        dec[:, h, 0, :] = np.exp((nni - mmi) * lf)
        dec[:, h, 1, :] = np.exp((mmi - nni) * lb_)
        for j in range(4):
            dl = nni - mmi - 128.0 * j
            dec[:, h, 2 + j, :] = np.where(dl >= 0, np.exp(dl * lf), np.exp(-dl * lb_))
    c["c_dec"] = dec.reshape(128, -1)
    s_ = np.arange(128)[:, None]
    t_ = np.arange(128)[None, :]
    same = (s_ // 64) == (t_ // 64)
    tri = np.zeros((128, 642), np.float32)
    tri[0:64, 258:386] = 1.0
    tri[64:128, 386:514] = 1.0
    tri[:, 0:128] = same & (s_ <= t_)
    tri[:, 128:256] = same & (s_ >= t_)
    tri[:, 256] = (np.arange(128) < 64)
    tri[:, 257] = (np.arange(128) >= 64)
    c["c_tri"] = tri
    cc = np.arange(128)[:, None] * np.arange(128)[None, :]
    phi = 2.0 * np.pi * (cc % 128) / 128.0
    c["c_cs"] = np.concatenate([np.cos(phi), -np.sin(phi)], 1).astype(bf)
    pp = (np.arange(S, dtype=np.int64)[:, None] * np.arange(S, dtype=np.int64)[None, :]) % S
    th = 2.0 * np.pi * pp / S
    sc = 1.0 / np.sqrt(S * 128.0)
    c["c_dft"] = np.stack([np.cos(th) * sc, np.sin(th) * sc]).astype(bf)
    return c
```
